# Optimizing a Trainium2 kernel written in Bass

```python
import jax, jax.numpy as jnp
from jax import lax
import numpy as np

D_MODEL = 1024
BATCH = 16
SEQ = 2048
DEPTH = 4
DEC_BATCH = 128
DEC_SEQ = 1
PAST_LEN = 8192
PAGE_SIZE = 128

D_MIX = D_MODEL
CONV_DIM = D_MIX // 2
CONV_GROUPS = 8
CONV_WIDTH = 3
N_HEADS = 8
QK_NOPE = 64
QK_ROPE = 32
V_HEAD = 64
ATTN_DIM = N_HEADS * V_HEAD
Q_LORA = D_MODEL // 4
KV_LORA = D_MODEL // 8
D_FF = 2816
ROPE_THETA = 10000.0
RMS_EPS = 1e-6
Q_BLOCK = 128
IN_COLS = 3 * CONV_DIM + Q_LORA + KV_LORA + QK_ROPE
SPLITS = (CONV_DIM, 2 * CONV_DIM, 3 * CONV_DIM, 3 * CONV_DIM + Q_LORA,
          3 * CONV_DIM + Q_LORA + KV_LORA)

kernel_name = 'hymba_shortconv_mla_macaron_step'


def rmsnorm(x, g):
    xf = x.astype(jnp.float32)
    y = xf * lax.rsqrt(jnp.mean(xf * xf, axis=-1, keepdims=True) + RMS_EPS)
    return (y * g.astype(jnp.float32)).astype(x.dtype)


def rope(x, pos):
    half = x.shape[-1] // 2
    inv_freq = 1.0 / (ROPE_THETA ** (jnp.arange(half, dtype=jnp.float32) / half))
    ang = pos.astype(jnp.float32)[:, None] * inv_freq[None, :]
    cos = jnp.cos(ang)[:, None, :]
    sin = jnp.sin(ang)[:, None, :]
    xf = x.astype(jnp.float32)
    x1, x2 = xf[..., :half], xf[..., half:]
    return jnp.concatenate([x1 * cos - x2 * sin, x2 * cos + x1 * sin], axis=-1).astype(x.dtype)


def swiglu(h, wg, wu, wd):
    return (jax.nn.silu(h @ wg) * (h @ wu)) @ wd


def mixer_project(h, w_in, q_a_norm, w_q_b, kv_a_norm, pos):
    b, s, _ = h.shape
    proj = h @ w_in
    b_g, c_g, xc, q_c, kv_c, k_pe = jnp.split(proj, SPLITS, axis=-1)
    u = c_g * xc
    q = (rmsnorm(q_c, q_a_norm) @ w_q_b).reshape(b, s, N_HEADS, QK_NOPE + QK_ROPE)
    q_nope = q[..., :QK_NOPE]
    q_pe = rope(q[..., QK_NOPE:], pos)
    ckv = rmsnorm(kv_c, kv_a_norm)
    kpe = rope(k_pe[:, :, None, :], pos)[:, :, 0, :]
    return b_g, u, q_nope, q_pe, ckv, kpe


def short_conv(u_hist, conv_w, s):
    return sum(conv_w[k] * u_hist[:, k:k + s] for k in range(CONV_WIDTH))


def mixer_output(conv_y, attn_o, gn_conv, gn_attn, w_o):
    merged = jnp.concatenate([rmsnorm(conv_y, gn_conv), rmsnorm(attn_o, gn_attn)], axis=-1)
    return merged @ w_o


def prompt_mla(q_nope, q_pe, ckv, kpe, w_kv_b):
    b, s = ckv.shape[0], ckv.shape[1]
    kv = (ckv @ w_kv_b).reshape(b, s, N_HEADS, QK_NOPE + V_HEAD)
    k_nope, v = kv[..., :QK_NOPE], kv[..., QK_NOPE:]
    scale = (QK_NOPE + QK_ROPE) ** -0.5
    nb = s // Q_BLOCK
    qn_b = q_nope.reshape(b, nb, Q_BLOCK, N_HEADS, QK_NOPE).transpose(1, 0, 2, 3, 4)
    qp_b = q_pe.reshape(b, nb, Q_BLOCK, N_HEADS, QK_ROPE).transpose(1, 0, 2, 3, 4)
    key_pos = jnp.arange(s, dtype=jnp.int32)

    def block(args):
        qn, qp, i = args
        sc = (jnp.einsum('bqhd,bkhd->bhqk', qn, k_nope).astype(jnp.float32)
              + jnp.einsum('bqhr,bkr->bhqk', qp, kpe).astype(jnp.float32)) * scale
        q_pos = i * Q_BLOCK + jnp.arange(Q_BLOCK, dtype=jnp.int32)
        mask = key_pos[None, :] <= q_pos[:, None]
        sc = jnp.where(mask[None, None], sc, -jnp.inf)
        p = jax.nn.softmax(sc, axis=-1).astype(v.dtype)
        return jnp.einsum('bhqk,bkhv->bqhv', p, v)

    o = lax.map(block, (qn_b, qp_b, jnp.arange(nb, dtype=jnp.int32)))
    return o.transpose(1, 0, 2, 3, 4).reshape(b, s, ATTN_DIM)


def sample_mla(q_nope, q_pe, ckv_new, kpe_new, ckv_past, kpe_past, w_kv_b):
    b, s = ckv_new.shape[0], ckv_new.shape[1]
    past = ckv_past.shape[1]
    w = w_kv_b.reshape(KV_LORA, N_HEADS, QK_NOPE + V_HEAD)
    w_uk, w_uv = w[..., :QK_NOPE], w[..., QK_NOPE:]
    scale = (QK_NOPE + QK_ROPE) ** -0.5
    q_lat = jnp.einsum('bshn,chn->bshc', q_nope, w_uk)
    s_past = (jnp.einsum('bshc,btc->bhst', q_lat, ckv_past).astype(jnp.float32)
              + jnp.einsum('bshr,btr->bhst', q_pe, kpe_past).astype(jnp.float32))
    s_new = (jnp.einsum('bshc,btc->bhst', q_lat, ckv_new).astype(jnp.float32)
             + jnp.einsum('bshr,btr->bhst', q_pe, kpe_new).astype(jnp.float32))
    causal = jnp.tril(jnp.ones((s, s), dtype=bool))
    s_new = jnp.where(causal[None, None], s_new, -jnp.inf)
    p = jax.nn.softmax(jnp.concatenate([s_past, s_new], axis=-1) * scale, axis=-1).astype(ckv_new.dtype)
    o_lat = (jnp.einsum('bhst,btc->bshc', p[..., :past], ckv_past)
             + jnp.einsum('bhst,btc->bshc', p[..., past:], ckv_new))
    return jnp.einsum('bshc,chv->bshv', o_lat, w_uv).reshape(b, s, ATTN_DIM)


def setup_inputs(seed: int = 0) -> dict:
    key = jax.random.key(seed)
    ks = jax.random.split(key, 32)
    n_pages = PAST_LEN // PAGE_SIZE
    n_pool = (DEC_BATCH * n_pages * 5) // 4

    def dense(k, shape, fan_in):
        return jax.random.normal(k, shape, jnp.float32) * (fan_in ** -0.5)

    def gain(k, shape):
        return 1.0 + 0.01 * jax.random.normal(k, shape, jnp.float32)

    page_table = jax.random.permutation(ks[5], n_pool)[:DEC_BATCH * n_pages]
    page_table = page_table.reshape(DEC_BATCH, n_pages).astype(jnp.int32)
    return {
        'x_prompt': jax.random.normal(ks[0], (BATCH, SEQ, D_MODEL), jnp.float32),
        'x_sample': jax.random.normal(ks[1], (DEC_BATCH, DEC_SEQ, D_MODEL), jnp.float32),
        'state_conv': jax.random.normal(ks[2], (DEPTH, DEC_BATCH, CONV_WIDTH - 1, CONV_DIM), jnp.float32),
        'cache_ckv': jax.random.normal(ks[3], (DEPTH, n_pool, PAGE_SIZE, KV_LORA), jnp.float32),
        'cache_kpe': jax.random.normal(ks[4], (DEPTH, n_pool, PAGE_SIZE, QK_ROPE), jnp.float32),
        'page_table': page_table,
        'norm_ffn1': gain(ks[6], (DEPTH, D_MODEL)),
        'w_ffn1_gate': dense(ks[7], (DEPTH, D_MODEL, D_FF), D_MODEL),
        'w_ffn1_up': dense(ks[8], (DEPTH, D_MODEL, D_FF), D_MODEL),
        'w_ffn1_down': dense(ks[9], (DEPTH, D_FF, D_MODEL), D_FF),
        'norm_mix': gain(ks[10], (DEPTH, D_MODEL)),
        'w_in': dense(ks[11], (DEPTH, D_MODEL, IN_COLS), D_MODEL),
        'conv_w': dense(ks[12], (DEPTH, CONV_WIDTH, CONV_DIM), CONV_WIDTH),
        'q_a_norm': gain(ks[13], (DEPTH, Q_LORA)),
        'w_q_b': dense(ks[14], (DEPTH, Q_LORA, N_HEADS * (QK_NOPE + QK_ROPE)), Q_LORA),
        'kv_a_norm': gain(ks[15], (DEPTH, KV_LORA)),
        'w_kv_b': dense(ks[16], (DEPTH, KV_LORA, N_HEADS * (QK_NOPE + V_HEAD)), KV_LORA),
        'gn_conv': gain(ks[17], (DEPTH, CONV_DIM)),
        'gn_attn': gain(ks[18], (DEPTH, ATTN_DIM)),
        'w_o': dense(ks[19], (DEPTH, D_MIX, D_MODEL), D_MIX),
        'norm_ffn2': gain(ks[20], (DEPTH, D_MODEL)),
        'w_ffn2_gate': dense(ks[21], (DEPTH, D_MODEL, D_FF), D_MODEL),
        'w_ffn2_up': dense(ks[22], (DEPTH, D_MODEL, D_FF), D_MODEL),
        'w_ffn2_down': dense(ks[23], (DEPTH, D_FF, D_MODEL), D_FF),
        'final_norm': gain(ks[24], (D_MODEL,)),
    }


def reference(x_prompt, x_sample, state_conv, cache_ckv, cache_kpe, page_table,
              norm_ffn1, w_ffn1_gate, w_ffn1_up, w_ffn1_down,
              norm_mix, w_in, conv_w, q_a_norm, w_q_b, kv_a_norm, w_kv_b,
              gn_conv, gn_attn, w_o,
              norm_ffn2, w_ffn2_gate, w_ffn2_up, w_ffn2_down, final_norm):
    pos_p = jnp.arange(SEQ, dtype=jnp.int32)
    pos_s = PAST_LEN + jnp.arange(DEC_SEQ, dtype=jnp.int32)
    yp, ys = x_prompt, x_sample
    conv_p, ckv_p, kpe_p, conv_s, ckv_s, kpe_s = [], [], [], [], [], []
    for l in range(DEPTH):
        yp = yp + 0.5 * swiglu(rmsnorm(yp, norm_ffn1[l]), w_ffn1_gate[l], w_ffn1_up[l], w_ffn1_down[l])
        ys = ys + 0.5 * swiglu(rmsnorm(ys, norm_ffn1[l]), w_ffn1_gate[l], w_ffn1_up[l], w_ffn1_down[l])

        h = rmsnorm(yp, norm_mix[l])
        b_g, u, qn, qp, ckv, kpe = mixer_project(h, w_in[l], q_a_norm[l], w_q_b[l], kv_a_norm[l], pos_p)
        u_hist = jnp.concatenate([jnp.zeros((BATCH, CONV_WIDTH - 1, CONV_DIM), u.dtype), u], axis=1)
        conv_y = b_g * short_conv(u_hist, conv_w[l], SEQ)
        attn_o = prompt_mla(qn, qp, ckv, kpe, w_kv_b[l])
        yp = yp + mixer_output(conv_y, attn_o, gn_conv[l], gn_attn[l], w_o[l])
        conv_p.append(u_hist[:, -(CONV_WIDTH - 1):])
        ckv_p.append(ckv)
        kpe_p.append(kpe)

        h = rmsnorm(ys, norm_mix[l])
        b_g, u, qn, qp, ckv, kpe = mixer_project(h, w_in[l], q_a_norm[l], w_q_b[l], kv_a_norm[l], pos_s)
        u_hist = jnp.concatenate([state_conv[l], u], axis=1)
        conv_y = b_g * short_conv(u_hist, conv_w[l], DEC_SEQ)
        ckv_past = cache_ckv[l][page_table].reshape(DEC_BATCH, -1, KV_LORA)
        kpe_past = cache_kpe[l][page_table].reshape(DEC_BATCH, -1, QK_ROPE)
        attn_o = sample_mla(qn, qp, ckv, kpe, ckv_past, kpe_past, w_kv_b[l])
        ys = ys + mixer_output(conv_y, attn_o, gn_conv[l], gn_attn[l], w_o[l])
        conv_s.append(u_hist[:, -(CONV_WIDTH - 1):])
        ckv_s.append(ckv)
        kpe_s.append(kpe)

        yp = yp + 0.5 * swiglu(rmsnorm(yp, norm_ffn2[l]), w_ffn2_gate[l], w_ffn2_up[l], w_ffn2_down[l])
        ys = ys + 0.5 * swiglu(rmsnorm(ys, norm_ffn2[l]), w_ffn2_gate[l], w_ffn2_up[l], w_ffn2_down[l])

    y_prompt = rmsnorm(yp, final_norm)
    y_sample = rmsnorm(ys, final_norm)
    new_conv_prompt = jnp.stack(conv_p)
    new_ckv_prompt = jnp.stack(ckv_p)
    new_kpe_prompt = jnp.stack(kpe_p)
    new_conv_sample = jnp.stack(conv_s)
    new_ckv_sample = jnp.stack(ckv_s)
    new_kpe_sample = jnp.stack(kpe_s)
    return (y_prompt, y_sample, new_conv_prompt, new_ckv_prompt, new_kpe_prompt,
            new_conv_sample, new_ckv_sample, new_kpe_sample)
```

```python
import math
import os
from contextlib import ExitStack
import numpy as np
import concourse.bass as bass
import concourse.mybir as mybir
from concourse.bass_utils import run_bass_kernel_spmd

F32 = mybir.dt.float32
BF16 = mybir.dt.bfloat16
I32 = mybir.dt.int32
AF = mybir.ActivationFunctionType
ALU = mybir.AluOpType
AX = mybir.AxisListType
ENGS = ("pe", "act", "dve", "pool", "sp")

D = 1024
NFF = 22
SEQ = 2048
NS = 16
NPG = 64
EPS = 1e-6
VL = 47
SCALE = 96.0 ** -0.5


class Buf:
    __slots__ = ("name", "w", "r")

    def __init__(self, name):
        self.name = name
        self.w = None
        self.r = []


class Op:
    __slots__ = ("eng", "fn", "deps", "dma", "sig", "sem", "semval", "needed", "idx")

    def __init__(self, eng, fn, dma):
        self.eng = eng
        self.fn = fn
        self.dma = dma
        self.deps = []
        self.sig = None
        self.sem = None
        self.semval = None
        self.needed = False


class Prog:
    def __init__(self):
        self.ops = {e: [] for e in ENGS}
        self.semmap = {}
        self.semvals = []
        self.last_dma = {}
        self.finals = []
        self.pending_fence = {e: None for e in ENGS}

    def _add(self, eng, fn, reads, writes, dma, sem_buf=None):
        op = Op(eng, fn, dma)
        deps = []

        def dep_on(o, same_ok):
            if o is None or o is op:
                return
            if o.eng == eng and not o.dma and not (same_ok or dma):
                return
            deps.append(o)

        so = eng != "pe"
        for b in reads:
            dep_on(b.w, so)
        for b in writes:
            dep_on(b.w, so)
            for r in b.r:
                dep_on(r, so)
        f = self.pending_fence[eng]
        if f is not None:
            deps.extend(o for o in f if o is not None and not (o.eng == eng and not o.dma))
            self.pending_fence[eng] = None
        last = {}
        dd = []
        for o in deps:
            if o.dma:
                dd.append(o)
            else:
                p = last.get(o.eng)
                if p is None or o.idx > p.idx:
                    last[o.eng] = o
        deps = dd + list(last.values())
        op.deps = deps
        for o in deps:
            o.needed = True
        for b in reads:
            b.r.append(op)
        for b in writes:
            b.w = op
            b.r = []
        if dma:
            k = sem_buf.name
            if k not in self.semmap:
                self.semmap[k] = len(self.semvals)
                self.semvals.append(0)
            s = self.semmap[k]
            self.semvals[s] += 16
            op.sem = s
            op.semval = self.semvals[s]
            self.last_dma[s] = op
        op.idx = len(self.ops[eng])
        self.ops[eng].append(op)
        return op

    def op(self, eng, fn, reads=(), writes=()):
        return self._add(eng, fn, list(reads), list(writes), False)

    def dma(self, eng, fn, reads=(), writes=(), sem_buf=None, final=False):
        if sem_buf is None:
            sem_buf = (list(writes) + list(reads))[0]
        op = self._add(eng, fn, list(reads), list(writes), True, sem_buf)
        if final:
            self.finals.append(op)
        return op

    def fence(self):
        lst = []
        for e in ENGS:
            for o in reversed(self.ops[e]):
                if not o.dma:
                    lst.append(o)
                    break
        lst.extend(self.last_dma.values())
        for e in ENGS:
            self.pending_fence[e] = list(lst)

    def emit(self, nc, engmap, stack):
        psem = {e: stack.enter_context(nc.semaphore(f"p_{e}")) for e in ENGS}
        dsem = [stack.enter_context(nc.semaphore(f"d_{i}")) for i in range(len(self.semvals))]
        for e in ENGS:
            c = 0
            for o in self.ops[e]:
                if not o.dma and o.needed:
                    c += 1
                    o.sig = c
        for e in ENGS:
            eng = engmap[e]
            wp = {x: 0 for x in ENGS}
            wd = {}
            for o in self.ops[e]:
                np_, nd = {}, {}
                for d in o.deps:
                    if d.dma:
                        if wd.get(d.sem, 0) < d.semval:
                            nd[d.sem] = max(nd.get(d.sem, 0), d.semval)
                    elif wp[d.eng] < d.sig:
                        np_[d.eng] = max(np_.get(d.eng, 0), d.sig)
                for x, v in np_.items():
                    eng.wait_ge(psem[x], v)
                    wp[x] = v
                for s, v in nd.items():
                    eng.wait_ge(dsem[s], v)
                    wd[s] = v
                ins = o.fn(eng)
                if o.dma:
                    ins.then_inc(dsem[o.sem], 16)
                elif o.sig is not None:
                    ins.then_inc(psem[e], 1)
            if e == "sp":
                for o in self.finals:
                    if wd.get(o.sem, 0) < o.semval:
                        eng.wait_ge(dsem[o.sem], o.semval)
                        wd[o.sem] = o.semval


class Ring:
    def __init__(self, items):
        self.items = items
        self.i = 0

    def next(self):
        it = self.items[self.i % len(self.items)]
        self.i += 1
        return it


def build(DEPTH, NPOOL, NSEQ=2, do_sample=True):
    nc = bass.Bass("TRN2", target_bir_lowering=False)
    NV = DEPTH * VL + 8

    def din(name, shape, dt=F32):
        return nc.dram_tensor(name, shape, dt, kind="ExternalInput").ap()

    def dout(name, shape, dt=F32):
        return nc.dram_tensor(name, shape, dt, kind="ExternalOutput").ap()

    xp = din("xp", [NSEQ * SEQ, D])
    xs = din("xs", [NS, D])
    stc = din("stc", [DEPTH * NS * 2, 512])
    cckv_l = [din(f"cckv{l}", [NPOOL, 128 * 128]) for l in range(DEPTH)]
    ckpe_l = [din(f"ckpe{l}", [NPOOL, 128 * 32]) for l in range(DEPTH)]
    ptd = din("pt", [128, 8], I32)
    vecs_d = din("vecs", [128, NV])
    gattn_d = din("gattn", [DEPTH, 512])
    wgu_d = din("wgu", [DEPTH * 2 * NFF, 128, 2048])
    wd_d = din("wd", [DEPTH * 2 * 8, 128, NFF * 128])
    win_d = din("win", [DEPTH * 8, 128, 2048])
    wqb_d = din("wqb", [DEPTH, 128, 2048])
    wkv_d = din("wkv", [DEPTH, 128, 1024])
    wo_d = din("wo", [DEPTH * 4, 128, 2048])
    yp = dout("yp", [NSEQ * SEQ, D])
    ys = dout("ys", [NS, D])
    ocp = dout("ocp", [DEPTH * NSEQ * 2, 512])
    ockv = dout("ockv", [DEPTH * NSEQ * SEQ, 128])
    okpe = dout("okpe", [DEPTH * NSEQ * SEQ, 32])
    ocs = dout("ocs", [DEPTH * NS * 2, 512])
    ockvs = dout("ockvs", [DEPTH * NS, 128])
    okpes = dout("okpes", [DEPTH * NS, 32])
    sck = nc.dram_tensor("sck", [DEPTH * NS * NPG, 128 * 128], F32).ap()
    skp = nc.dram_tensor("skp", [DEPTH * NS * NPG, 128 * 32], F32).ap()

    pr = Prog()
    st = ExitStack()
    with st:
        def sb(name, shape, dt):
            return st.enter_context(nc.sbuf_tensor("s_" + name, shape, dt))

        def psb(name, shape, dt=F32):
            return st.enter_context(nc.psum_tensor(name, shape, dt))

        x = sb("x", [128, 8, 1040], F32)
        h = sb("h", [128, 8, 1040], BF16)
        wring_t = sb("wring", [128, 3, 2048], BF16)
        wdring_t = sb("wdring", [128, 2, NFF * 128], BF16)
        wqb_t = sb("wqbt", [128, 2048], BF16)
        wkv_t = sb("wkvt", [128, 1024], BF16)
        bwqb, bwkv = Buf("wqb"), Buf("wkv")
        vecs = sb("vecs", [128, NV], F32)
        gattn = sb("gattn", [128, 512], F32)
        ident = sb("ident", [128, 128], BF16)
        identf = sb("identf", [128, 128], F32)
        ones = sb("ones", [128, 4, 128], BF16)
        maskneg = sb("maskneg", [128, 128], F32)
        masks = sb("masks", [128, NS], F32)
        cst = sb("cst", [128, 8], F32)
        ctab = sb("ctab", [128, 1024], F32)
        stab = sb("stab", [128, 1024], F32)
        kct_s = sb("kct_s", [128, DEPTH, 1024], BF16)
        kp4_s = sb("kp4_s", [128, DEPTH, 1024], BF16)
        vn_s = sb("vn_s", [128, DEPTH, 8, 128], BF16)
        uh = sb("uh", [128, DEPTH, 4, 2], F32)
        pti = sb("pti", [128, 8], I32)
        tmi = sb("tmi", [128, 2], I32)
        idx8 = sb("idx8", [128, 10, 8], I32)
        idxf = sb("idxf", [128, 2, 8], F32)
        gstg = sb("gstg", [128, 2048], F32) if do_sample else None
        ARENA = 48000 if not do_sample else 42100
        arena = sb("arena", [128, ARENA], BF16)

        class Carver:
            def __init__(self):
                self.off = 0

            def get(self, shape, dt):
                n = int(np.prod(shape))
                nb = n * (4 if dt in (F32, I32) else 2)
                nb = (nb + 31) // 32 * 32
                a = arena[:, self.off // 2:(self.off + nb) // 2]
                self.off += nb
                assert self.off <= ARENA * 2, (self.off, ARENA * 2)
                if dt != BF16:
                    a = a.bitcast(dt)
                a = a[:, 0:n]
                if len(shape) == 1:
                    return a
                if len(shape) == 2:
                    return a.rearrange("p (a b) -> p a b", a=shape[0])
                if len(shape) == 3:
                    return a.rearrange("p (a b c) -> p a b c", a=shape[0], b=shape[1])
                raise ValueError

        psA = [psb(f"psA{i}", [128, 512]) for i in range(4)]
        psS = psb("psS", [128, 2048])
        bA = [Buf(f"psA{i}") for i in range(4)]
        bS = [Buf(f"psS{i}") for i in range(4)]
        ring3 = Ring([(psA[i], bA[i]) for i in range(3)])
        psO, bO = psA[3], bA[3]
        ring8 = Ring([(psA[i], bA[i]) for i in range(4)] + [(psS[:, i * 512:(i + 1) * 512], bS[i]) for i in range(4)])
        psS_b = psS.bitcast(BF16) if hasattr(psS, "bitcast") else None

        bx = [[Buf(f"x{t}_{c}") for c in range(8)] for t in range(3)]
        bh = [[Buf(f"h{t}_{c}") for c in range(8)] for t in range(3)]
        wring = Ring([(wring_t[:, i, :], Buf(f"wr{i}")) for i in range(3)])
        wdring = Ring([(wdring_t[:, i, :], Buf(f"wdr{i}")) for i in range(2)])
        bconst = Buf("const")
        btab = Buf("tab")
        bkv_s = [Buf(f"kvs{l}") for l in range(DEPTH)]
        buh = [Buf(f"uh{l}") for l in range(DEPTH)]

        G = {}
        TILES_P = [(0, 512), (512, 512)]
        TS = (1024, 16)

        def V(l, k, n=1):
            c = l * VL + k
            return vecs[:, c:c + n]

        def mm(out, lhsT, rhs, start, stop, reads, writes):
            pr.op("pe", lambda e: e.matmul(out, lhsT=lhsT, rhs=rhs, start=start, stop=stop), reads, writes)

        def tr(out, in_, idn, reads, writes):
            pr.op("pe", lambda e: e.transpose(out, in_, idn), reads, writes)

        def act(out, in_, func, reads, writes, bias=None, scale=1.0, accum=None):
            kw = {}
            if bias is not None:
                kw["bias"] = bias
            if accum is not None:
                kw["accum_out"] = accum
            pr.op("act", lambda e: e.activation(out=out, in_=in_, func=func, scale=scale, **kw), reads, writes)

        def tt(out, in0, in1, op, reads, writes, eng="dve"):
            pr.op(eng, lambda e: e.tensor_tensor(out=out, in0=in0, in1=in1, op=op), reads, writes)

        def ts(out, in0, s1, op0, reads, writes, s2=None, op1=None, eng="dve"):
            if op1 is None:
                pr.op(eng, lambda e: e.tensor_scalar(out=out, in0=in0, scalar1=s1, scalar2=None, op0=op0), reads, writes)
            else:
                pr.op(eng, lambda e: e.tensor_scalar(out=out, in0=in0, scalar1=s1, scalar2=s2, op0=op0, op1=op1), reads, writes)

        def stt(out, in0, scalar, in1, op0, op1, reads, writes, eng="dve"):
            pr.op(eng, lambda e: e.scalar_tensor_tensor(out=out, in0=in0, scalar=scalar, in1=in1, op0=op0, op1=op1), reads, writes)

        def cp(out, in_, reads, writes, eng="dve"):
            if eng == "act":
                pr.op("act", lambda e: e.activation(out=out, in_=in_, func=AF.Copy), reads, writes)
            else:
                pr.op(eng, lambda e: e.tensor_copy(out=out, in_=in_), reads, writes)

        def recip(out, in_, reads, writes):
            pr.op("dve", lambda e: e.reciprocal(out=out, in_=in_), reads, writes)

        def load(eng, out, in_, wbuf, final=False, reads=()):
            pr.dma(eng, lambda e: e.dma_start(out=out, in_=in_), reads=reads, writes=[wbuf])

        def store(out, in_, rbuf, eng="sp"):
            pr.dma(eng, lambda e: e.dma_start(out=out, in_=in_), reads=[rbuf], final=True)

        def store_nc(out, in_, rbuf, eng="sp"):
            pr.dma(eng, lambda e: e.dma_start(out=out, in_=in_, allow_slow_non_contiguous=True), reads=[rbuf], final=True)

        pending = []
        bgstg = Buf("gstg")
        bidx = Buf("idx")
        bgath = [Buf(f"gath{l}") for l in range(DEPTH)]

        def gather_one():
            if not pending:
                return
            l, kind, g, q = pending.pop(0)
            if kind == 0:
                src = cckv_l[l].rearrange("n (q c) -> (n q) c", q=8)
                eo = 0
                dst = sck[(l * 8 + g) * 128:(l * 8 + g + 1) * 128, q * 2048:(q + 1) * 2048]
                ixa = idx8[:, q, g:g + 1]
            else:
                src = ckpe_l[l].rearrange("n (q c) -> (n q) c", q=2)
                eo = 0
                dst = skp[(l * 8 + g) * 128:(l * 8 + g + 1) * 128, q * 2048:(q + 1) * 2048]
                ixa = idx8[:, 8 + q, g:g + 1]
            pr.dma("pool", lambda e: e.indirect_dma_start(out=gstg[:, :], out_offset=None, in_=src,
                                                          in_offset=bass.IndirectOffsetOnAxis(ap=ixa, axis=0)),
                   reads=[bidx], writes=[bgstg])
            bgath[l].w = pr.dma("sp", lambda e: e.dma_start(out=dst, in_=gstg[:, :]), reads=[bgstg], writes=[], sem_buf=bgath[l])

        def wload(dram_row, n=2048):
            gather_one()
            wap, wb = wring.next()
            load("pool", wap[:, 0:n], dram_row, wb)
            return wap, wb

        load("sp", vecs[:], vecs_d[:, :], bconst)
        pr.op("pool", lambda e: e.memset(identf[:], 0.0), [], [bconst])
        pr.op("pool", lambda e: e.affine_select(out=identf[:], in_=identf[:], pattern=[[-1, 128]], compare_op=ALU.not_equal,
                                                fill=1.0, base=0, channel_multiplier=1), [bconst], [bconst])
        cp(ident[:], identf[:], [bconst], [bconst])
        for i, dv in enumerate((1024, 512, 256, 128)):
            pr.op("pool", lambda e, i=i, dv=dv: e.memset(ones[:, i, :], 1.0 / dv), [], [bconst])
        pr.op("pool", lambda e: e.memset(maskneg[:], 0.0), [], [bconst])
        pr.op("pool", lambda e: e.affine_select(out=maskneg[:], in_=maskneg[:], pattern=[[-1, 128]], compare_op=ALU.is_ge,
                                                fill=-1e30, base=0, channel_multiplier=1), [bconst], [bconst])
        pr.op("pool", lambda e: e.memset(masks[:], 0.0), [], [bconst])
        pr.op("pool", lambda e: e.affine_select(out=masks[:], in_=masks[:], pattern=[[-8, NS]], compare_op=ALU.is_ge,
                                                fill=-1e30, base=0, channel_multiplier=1), [bconst], [bconst])
        pr.op("pool", lambda e: e.affine_select(out=masks[:], in_=masks[:], pattern=[[8, NS]], compare_op=ALU.is_ge,
                                                fill=-1e30, base=7, channel_multiplier=-1), [bconst], [bconst])
        pr.op("pool", lambda e: e.memset(cst[:, 0:1], EPS), [], [bconst])
        pr.op("pool", lambda e: e.memset(cst[:, 1:2], -math.pi), [], [bconst])
        C1 = 6.28125
        C2 = 2 * math.pi - C1

        def reduce_mod(dst, src, period, lo, hi, tmpi, tmpm, rd, wr):
            ts(dst, src, 1.0 / period, ALU.mult, rd, wr)
            cp(tmpi, dst, wr, wr)
            cp(dst, tmpi, wr, wr)
            if abs(period - 2 * math.pi) < 1e-9:
                stt(tmpm, dst, -C1, src, ALU.mult, ALU.add, rd + wr, wr)
                stt(dst, dst, -C2, tmpm, ALU.mult, ALU.add, wr, wr)
            else:
                stt(dst, dst, -float(period), src, ALU.mult, ALU.add, rd + wr, wr)
            ts(tmpm, dst, float(hi), ALU.is_ge, wr, wr, s2=float(period), op1=ALU.mult)
            tt(dst, dst, tmpm, ALU.subtract, wr, wr)
            ts(tmpm, dst, float(lo), ALU.is_lt, wr, wr, s2=float(period), op1=ALU.mult)
            tt(dst, dst, tmpm, ALU.add, wr, wr)

        pr.op("pool", lambda e: e.iota(out=tmi[:, 1:2], pattern=[[0, 1]], base=0, channel_multiplier=1), [], [bconst])
        cp(cst[:, 6:7], tmi[:, 1:2], [bconst], [bconst])
        reduce_mod(cst[:, 7:8], cst[:, 6:7], 16.0, 0.0, 16.0, tmi[:, 0:1], cst[:, 3:4], [bconst], [bconst])
        act(cst[:, 2:3], cst[:, 7:8], AF.Exp, [bconst], [bconst], scale=-math.log(10000.0) / 16.0)
        reduce_mod(cst[:, 7:8], cst[:, 6:7], 32.0, 0.0, 32.0, tmi[:, 0:1], cst[:, 3:4], [bconst], [bconst])
        ts(cst[:, 3:4], cst[:, 7:8], 16.0, ALU.is_ge, [bconst], [bconst], s2=2.0, op1=ALU.mult)
        ts(cst[:, 3:4], cst[:, 3:4], -1.0, ALU.add, [bconst], [bconst])

        def make_tables(ct, stt_, base, n, tmpi, tmpf, tmpm, btab=btab):
            pr.op("pool", lambda e: e.iota(out=tmpi, pattern=[[1, n]], base=base, channel_multiplier=0), [], [btab])
            cp(tmpf, tmpi, [btab], [btab])
            ts(tmpf, tmpf, cst[:, 2:3], ALU.mult, [btab, bconst], [btab])
            reduce_mod(stt_, tmpf, 2 * math.pi, -math.pi, math.pi, tmpi, tmpm, [btab], [btab])
            act(stt_, stt_, AF.Sin, [btab], [btab])
            ts(stt_, stt_, cst[:, 3:4], ALU.mult, [btab, bconst], [btab])
            ts(tmpf, tmpf, math.pi / 2, ALU.add, [btab], [btab])
            reduce_mod(ct, tmpf, 2 * math.pi, -math.pi, math.pi, tmpi, tmpm, [btab], [btab])
            act(ct, ct, AF.Sin, [btab], [btab])

        def rms_stats(srcs, w, onesi, reads):
            ps, pb = G['ring'].next()
            n = len(srcs)
            for i, s_ in enumerate(srcs):
                sq, sqb = G['sq'].next()
                act(sq[:, 0:w], s_, AF.Square, reads[i], [sqb])
                mm(ps[:, 0:w], ones[:, onesi, :], sq[:, 0:w], i == 0, i == n - 1, [sqb, bconst], [pb])
            rt, rtb = G['rt'].next()
            act(rt[:, 0:w], ps[:, 0:w], AF.Sqrt, [pb, bconst], [rtb], bias=cst[:, 0:1])
            recip(rt[:, 0:w], rt[:, 0:w], [rtb], [rtb])
            return rt, rtb

        def norm_to_h(tiles, l, voff):
            for (ti, (c0, w)) in tiles:
                rt, rtb = rms_stats([x[:, kc, c0:c0 + w] for kc in range(8)], w, 0, [[bx[ti][kc]] for kc in range(8)])
                for kc in range(8):
                    stt(h[:, kc, c0:c0 + w], x[:, kc, c0:c0 + w], V(l, voff + kc), rt[:, 0:w], ALU.mult, ALU.mult,
                        [bx[ti][kc], rtb, bconst], [bh[ti][kc]])

        def ffn(tiles, l, f):
            G['ring'] = ring8
            cv = Carver()
            A = cv.get([NFF, 1040], BF16)
            sgs = [cv.get([512], F32) for _ in range(2)]
            G['sq'] = Ring([(cv.get([512], BF16), Buf(f"sq{i}")) for i in range(2)])
            G['rt'] = Ring([(cv.get([512], F32), Buf(f"rt{i}")) for i in range(2)])
            sgring = Ring([(sgs[i], Buf(f"sg{i}")) for i in range(2)])
            bA = [[Buf(f"A{t}_{j}") for j in range(NFF)] for t in range(3)]
            norm_to_h(tiles, l, 0 if f == 0 else 16)
            for j in range(NFF):
                wap, wb = wload(wgu_d[(l * 2 + f) * NFF + j])
                wv = wap.rearrange("p (g k m) -> p g k m", g=2, k=8)
                for (ti, (c0, w)) in tiles:
                    gp, gb = G['ring'].next()
                    up, ub = G['ring'].next()
                    for kc in range(8):
                        mm(gp[:, 0:w], wv[:, 0, kc, :], h[:, kc, c0:c0 + w], kc == 0, kc == 7, [wb, bh[ti][kc]], [gb])
                    for kc in range(8):
                        mm(up[:, 0:w], wv[:, 1, kc, :], h[:, kc, c0:c0 + w], kc == 0, kc == 7, [wb, bh[ti][kc]], [ub])
                    sg, sgb = sgring.next()
                    act(sg[:, 0:w], gp[:, 0:w], AF.Silu, [gb], [sgb])
                    tt(A[:, j, c0:c0 + w], sg[:, 0:w], up[:, 0:w], ALU.mult, [sgb, ub], [bA[ti][j]])
            for c in range(8):
                wap, wb = wdring.next()
                load("pool", wap, wd_d[(l * 2 + f) * 8 + c], wb)
                wv = wap.rearrange("p (j m) -> p j m", j=NFF)
                for (ti, (c0, w)) in tiles:
                    dp, db = G['ring'].next()
                    for j in range(NFF):
                        mm(dp[:, 0:w], wv[:, j, :], A[:, j, c0:c0 + w], j == 0, j == NFF - 1, [wb, bA[ti][j]], [db])
                    stt(x[:, c, c0:c0 + w], dp[:, 0:w], 0.5, x[:, c, c0:c0 + w], ALU.mult, ALU.add, [db, bx[ti][c]], [bx[ti][c]])
            pr.fence()

        def split3(src, parts, rd, bsrc, bparts):
            cp(parts[0], src, rd + [bsrc], [bparts], eng="act")
            tt(src, src, parts[0], ALU.subtract, [bsrc, bparts], [bsrc])
            cp(parts[1], src, [bsrc], [bparts], eng="act")
            tt(src, src, parts[1], ALU.subtract, [bsrc, bparts], [bsrc])
            cp(parts[2], src, [bsrc], [bparts], eng="act")

        def xT(out, lhs_parts, K, rd, wr):
            for i in range(3):
                mm(out, lhs_parts[i], ident[0:K, 0:K], i == 0, i == 2, rd + [bconst], wr)

        def load_x(tiles_rows):
            G['ring'] = ring8
            cv = Carver()
            stg = Ring([(cv.get([1024], F32), Buf(f"stg{i}")) for i in range(2)])
            spl = Ring([(cv.get([3, 1024], BF16), Buf(f"spl{i}")) for i in range(2)])
            for (ti, c0, dap, rows) in tiles_rows:
                sg, sgb = stg.next()
                load("sp", sg[0:rows, :], dap, sgb)
                sp_, spb = spl.next()
                split3(sg[0:rows, :], [sp_[0:rows, i, :] for i in range(3)], [], sgb, spb)
                for kc in range(8):
                    ps, pb = G['ring'].next()
                    xT(ps[:, 0:rows], [sp_[0:rows, i, kc * 128:(kc + 1) * 128] for i in range(3)], rows, [spb], [pb])
                    cp(x[:, kc, c0:c0 + rows], ps[:, 0:rows], [pb], [bx[ti][kc]], eng="act" if kc % 2 else "dve")
            pr.fence()

        def store_y(tiles_rows):
            G['ring'] = ring8
            cv = Carver()
            G['sq'] = Ring([(cv.get([512], BF16), Buf(f"sq{i}")) for i in range(2)])
            G['rt'] = Ring([(cv.get([512], F32), Buf(f"rt{i}")) for i in range(2)])
            stg = Ring([(cv.get([1024], F32), Buf(f"stg{i}")) for i in range(2)])
            yn = cv.get([8, 512], F32)
            ysp = cv.get([3, 8, 512], BF16)
            byn = Buf("yn"); bysp = Buf("ysp")
            done = set()
            for (ti, c0, dap, rows) in tiles_rows:
                tc0, tw = (TILES_P + [TS])[ti]
                if ti not in done:
                    done.add(ti)
                    rt, rtb = rms_stats([x[:, kc, tc0:tc0 + tw] for kc in range(8)], tw, 0, [[bx[ti][kc]] for kc in range(8)])
                    for kc in range(8):
                        stt(yn[:, kc, 0:tw], x[:, kc, tc0:tc0 + tw], vecs[:, DEPTH * VL + kc:DEPTH * VL + kc + 1], rt[:, 0:tw],
                            ALU.mult, ALU.mult, [bx[ti][kc], rtb, bconst], [byn])
                    split3(yn[:, :, 0:tw], [ysp[:, i, :, 0:tw] for i in range(3)], [], byn, bysp)
                sg, sgb = stg.next()
                o = c0 - tc0
                for kc in range(8):
                    ps, pb = G['ring'].next()
                    xT(ps[0:rows, 0:128], [ysp[:, i, kc, o:o + rows] for i in range(3)], 128, [bysp], [pb])
                    cp(sg[0:rows, kc * 128:(kc + 1) * 128], ps[0:rows, 0:128], [pb], [sgb], eng="act" if kc % 2 else "dve")
                store(dap, sg[0:rows, :], sgb)
            pr.fence()

        def mixer(l, seq, half, smp_phase):
            G['ring'] = ring3
            base = half * 1024
            W = NS if smp_phase else 512
            cv = Carver()
            cv_s = cv
            G['sq'] = Ring([(cv.get([W], BF16), Buf(f"sq{i}")) for i in range(2)])
            G['rt'] = Ring([(cv.get([W], F32), Buf(f"rt{i}")) for i in range(1)])
            cgr = Ring([(cv.get([W], F32), Buf(f"cg{i}")) for i in range(1)])
            ur = Ring([(cv.get([W + 2], F32), Buf(f"u{i}")) for i in range(1)])
            t1 = cv.get([4, W], F32); bt1 = [Buf(f"t1_{c}") for c in range(4)]
            qc = cv.get([2, W], F32); bqc = Buf("qc")
            qn = cv.get([2, W], BF16); bqn = Buf("qn")
            qnope = cv.get([4, W], BF16); bqnope = Buf("qnope")
            QL = cv.get([8, W], BF16); bQL = Buf("QL")
            ra = cv.get([W], F32); rb = cv.get([W], F32); brab = Buf("rab")
            rq = cv.get([2, W], BF16); brq = Buf("rq")
            rq3 = cv.get([2, W], BF16)
            ckvf = cv.get([W], F32); bckvf = Buf("ckvf")
            kpf = cv.get([W], F32); kpb_ = cv.get([W], F32); bkpf = Buf("kpf")
            pr.op("dve", lambda e: e.memset(kpf[:, :], 0.0), [], [bkpf])
            merged = cv.get([8, W], BF16); bmg = [Buf(f"mg{c}") for c in range(8)]
            stat = cv.get([32], F32); bstat = Buf("stat")
            bgat = Buf("gat")
            import os
            KX = int(os.environ.get("KX", "0"))
            if not KX & 1:
                load("sp", gattn[:, :], gattn_d[l:l + 1, :].partition_broadcast(128), bgat)
            if not smp_phase:
                ostg = cv.get([4, 128], F32); bostg = Buf("ostg")
                kstg = cv.get([4, 32], F32); bkstg = Buf("kstg")
                csp = cv.get([3, 512], BF16); bcsp = Buf("csp")
                ksp = cv.get([3, 512], BF16); bksp = Buf("ksp")
                Pr = Ring([(cv.get([2048], BF16), Buf(f"P{i}")) for i in range(1)])
                PT = cv.get([16, 128], BF16); bPT = Buf("PT")
                olT = Ring([(cv.get([128], BF16), Buf(f"olT{i}")) for i in range(2)])
                Osb = cv.get([512], F32); bOsb = Buf("Osb")
                On = cv.get([512], BF16); bOn = Buf("On")
                if half == 0:
                    kct_c, kp4_c, vn_c, bkv_c = kct_s[:, l, :], kp4_s[:, l, :], vn_s[:, l, :, :], bkv_s[l]
                else:
                    kct_c = cv.get([1024], BF16); kp4_c = cv.get([1024], BF16); vn_c = cv.get([8, 128], BF16)
                    bkv_c = Buf("kvc")
            else:
                ustg = cv.get([512], F32); bustg = Buf("ustg")
                ucp = cv.get([NS], F32); bucp = Buf("ucp")
                usp = cv.get([3, NS], BF16); busp = Buf("usp")
                kct_c = kp4_c = vn_c = bkv_c = None

            def kct(k0, n):
                return (kct_s[:, l, k0:k0 + n], bkv_s[l]) if k0 < 1024 and half == 1 else (kct_c[:, k0 - base:k0 - base + n], bkv_c)

            def kp4(k0, n):
                return (kp4_s[:, l, k0:k0 + n], bkv_s[l]) if k0 < 1024 and half == 1 else (kp4_c[:, k0 - base:k0 - base + n], bkv_c)

            def vn(kt):
                return (vn_s[:, l, kt, :], bkv_s[l]) if kt < 8 and half == 1 else (vn_c[:, kt - base // 128, :], bkv_c)

            wqb_ap, wqb_b = wqb_t[:, :], bwqb
            load("pool", wqb_ap, wqb_d[l], wqb_b)
            wqb_v = wqb_ap.rearrange("p (k m) -> p k m", k=2)
            wkv_ap, wkv_b = wkv_t[:, :], bwkv
            load("pool", wkv_ap, wkv_d[l], wkv_b)
            wuk_v = wkv_ap[:, 0:512].rearrange("p (i c) -> p i c", i=4)
            wuv_v = wkv_ap[:, 512:1024]

            tiles = [(2, TS)] if smp_phase else [(0, TILES_P[0]), (1, TILES_P[1])]
            for (ti, (c0, w)) in tiles:
                smp = ti == 2
                norm_to_h([(ti, (c0, w))], l, 8)
                hh = lambda kc: h[:, kc, c0:c0 + w]
                rh = lambda kc: [bh[ti][kc]]

                def proj(wv, wb, col, M, ps, pb):
                    for kc in range(8):
                        mm(ps[0:M, 0:w], wv[:, kc, col:col + M], hh(kc), kc == 0, kc == 7, [wb] + rh(kc), [pb])

                if smp:
                    stT = cv_s.get([4, 32], F32); bst = Buf("stT")
                    stg32 = cv_s.get([512], F32); bstg32 = Buf("stg32")
                    load("sp", stg32[0:32, :], stc[l * 32:(l + 1) * 32, :], bstg32)
                    pr.dma("sp", lambda e: e.dma_start(out=ocs.rearrange("(r k) f -> r k f", k=2)[l * NS:(l + 1) * NS, 0, :],
                                                       in_=stc.rearrange("(r k) f -> r k f", k=2)[l * NS:(l + 1) * NS, 1, :]),
                           reads=[], writes=[Buf("ocs0")], final=True)
                    s32 = cv_s.get([3, 512], BF16); bs32 = Buf("s32")
                    split3(stg32[0:32, :], [s32[0:32, i, :] for i in range(3)], [], bstg32, bs32)
                    for c in range(4):
                        ps, pb = G['ring'].next()
                        xT(ps[:, 0:32], [s32[0:32, i, c * 128:(c + 1) * 128] for i in range(3)], 32, [bs32], [pb])
                        cp(stT[:, c, :], ps[:, 0:32], [pb], [bst])
                for g in range(4):
                    wap, wb = wload(win_d[l * 8 + g])
                    wv = wap.rearrange("p (k m) -> p k m", k=8)
                    ps, pb = G['ring'].next()
                    proj(wv, wb, 0, 128, ps, pb)
                    cg, cgb = cgr.next()
                    cp(cg[:, 0:w], ps[:, 0:w], [pb], [cgb], eng="act")
                    ps2, pb2 = G['ring'].next()
                    proj(wv, wb, 128, 128, ps2, pb2)
                    u, ub = ur.next()
                    if not smp:
                        if ti == 0 and half == 0:
                            pr.op("dve", lambda e, u=u: e.memset(u[:, 0:2], 0.0), [], [ub])
                        else:
                            cp(u[:, 0:2], uh[:, l, g, :], [buh[l]], [ub])
                        tt(u[:, 2:2 + w], ps2[:, 0:w], cg[:, 0:w], ALU.mult, [pb2, cgb], [ub])
                        cp(uh[:, l, g, :], u[:, w:w + 2], [ub], [buh[l]])
                        if half == 1 and ti == 1 and not KX & 2:
                            r0 = (l * NSEQ + seq) * 2
                            store_nc(ocp[r0:r0 + 2, g * 128:(g + 1) * 128].rearrange("k p -> p k"), u[:, w:w + 2], ub)
                        ts(t1[:, g, 0:w], u[:, 0:w], V(l, 31 + g), ALU.mult, [ub, bconst], [bt1[g]])
                        stt(t1[:, g, 0:w], u[:, 1:1 + w], V(l, 35 + g), t1[:, g, 0:w], ALU.mult, ALU.add, [ub, bconst, bt1[g]], [bt1[g]])
                        stt(t1[:, g, 0:w], u[:, 2:2 + w], V(l, 39 + g), t1[:, g, 0:w], ALU.mult, ALU.add, [ub, bconst, bt1[g]], [bt1[g]])
                    else:
                        tt(u[:, 0:w], ps2[:, 0:w], cg[:, 0:w], ALU.mult, [pb2, cgb], [ub])
                        stv = stT[:, g, :].rearrange("p (b k) -> p k b", k=2)
                        ts(t1[:, g, 0:w], stv[:, 0, :], V(l, 31 + g), ALU.mult, [bst, bconst], [bt1[g]])
                        stt(t1[:, g, 0:w], stv[:, 1, :], V(l, 35 + g), t1[:, g, 0:w], ALU.mult, ALU.add, [bst, bconst, bt1[g]], [bt1[g]])
                        stt(t1[:, g, 0:w], u[:, 0:w], V(l, 39 + g), t1[:, g, 0:w], ALU.mult, ALU.add, [ub, bconst, bt1[g]], [bt1[g]])
                        cp(ucp[:, 0:NS], u[:, 0:NS], [ub], [bucp])
                        split3(ucp[:, 0:NS], [usp[:, i, :] for i in range(3)], [], bucp, busp)
                        ps3, pb3 = G['ring'].next()
                        xT(ps3[0:NS, 0:128], [usp[:, i, :] for i in range(3)], 128, [busp], [pb3])
                        cp(ustg[0:NS, g * 128:(g + 1) * 128], ps3[0:NS, 0:128], [pb3], [bustg])
                if smp:
                    store(ocs.rearrange("(r k) f -> r k f", k=2)[l * NS:(l + 1) * NS, 1, :], ustg[0:NS, :], bustg)
                for g in range(2):
                    wap, wb = wload(win_d[l * 8 + 4 + g])
                    wv = wap.rearrange("p (k m) -> p k m", k=8)
                    for i in range(2):
                        c = g * 2 + i
                        ps, pb = G['ring'].next()
                        proj(wv, wb, i * 128, 128, ps, pb)
                        tt(t1[:, c, 0:w], t1[:, c, 0:w], ps[:, 0:w], ALU.mult, [bt1[c], pb], [bt1[c]])
                rt, rtb = rms_stats([t1[:, c, 0:w] for c in range(4)], w, 1, [[bt1[c]] for c in range(4)])
                for c in range(4):
                    stt(merged[:, c, 0:w], t1[:, c, 0:w], V(l, 27 + c), rt[:, 0:w], ALU.mult, ALU.mult, [bt1[c], rtb, bconst], [bmg[c]])
                wap, wb = wload(win_d[l * 8 + 6])
                wv = wap.rearrange("p (k m) -> p k m", k=8)
                for i in range(2):
                    ps, pb = G['ring'].next()
                    proj(wv, wb, i * 128, 128, ps, pb)
                    cp(qc[:, i, 0:w], ps[:, 0:w], [pb], [bqc], eng="act")
                rt, rtb = rms_stats([qc[:, i, 0:w] for i in range(2)], w, 2, [[bqc], [bqc]])
                for i in range(2):
                    stt(qn[:, i, 0:w], qc[:, i, 0:w], V(l, 24 + i), rt[:, 0:w], ALU.mult, ALU.mult, [bqc, rtb, bconst], [bqn])
                for i in range(4):
                    ps, pb = G['ring'].next()
                    for kc in range(2):
                        mm(ps[:, 0:w], wqb_v[:, kc, i * 128:(i + 1) * 128], qn[:, kc, 0:w], kc == 0, kc == 1, [wqb_b, bqn], [pb])
                    cp(qnope[:, i, 0:w], ps[:, 0:w], [pb], [bqnope], eng="act")
                if smp:
                    cT, sT = cst[:, 4:5].to_broadcast([128, w]), cst[:, 5:6].to_broadcast([128, w])
                    tabr = [bconst]
                else:
                    cT, sT = ctab[:, c0:c0 + w], stab[:, c0:c0 + w]
                    tabr = [btab]
                for i in range(2):
                    ps, pb = G['ring'].next()
                    ps2, pb2 = G['ring'].next()
                    for kc in range(2):
                        mm(ps[:, 0:w], wqb_v[:, kc, 512 + i * 128:512 + (i + 1) * 128], qn[:, kc, 0:w], kc == 0, kc == 1, [wqb_b, bqn], [pb])
                    for kc in range(2):
                        mm(ps2[:, 0:w], wqb_v[:, kc, 768 + i * 128:768 + (i + 1) * 128], qn[:, kc, 0:w], kc == 0, kc == 1, [wqb_b, bqn], [pb2])
                    tt(ra[:, 0:w], ps[:, 0:w], cT, ALU.mult, [pb] + tabr, [brab])
                    tt(rb[:, 0:w], ps2[:, 0:w], sT, ALU.mult, [pb2] + tabr, [brab])
                    tt(rq[:, i, 0:w], ra[:, 0:w], rb[:, 0:w], ALU.add, [brab], [brq])
                    cp(rq3[0:32, i, 0:w], rq[96:128, i, 0:w], [brq], [brq], eng="act")
                for hd in range(8):
                    ps, pb = G['ring'].next()
                    r0 = (hd % 2) * 64
                    mm(ps[:, 0:w], wuk_v[r0:r0 + 64, hd // 2, :], qnope[r0:r0 + 64, hd // 2, 0:w], True, True, [wkv_b, bqnope], [pb])
                    cp(QL[:, hd, 0:w], ps[:, 0:w], [pb], [bQL], eng="act" if hd % 2 else "dve")
                wap, wb = wload(win_d[l * 8 + 7])
                wv = wap.rearrange("p (k m) -> p k m", k=8)
                ps, pb = G['ring'].next()
                proj(wv, wb, 0, 128, ps, pb)
                cp(ckvf[:, 0:w], ps[:, 0:w], [pb], [bckvf], eng="act")
                rt, rtb = rms_stats([ckvf[:, 0:w]], w, 3, [[bckvf]])
                stt(ckvf[:, 0:w], ckvf[:, 0:w], V(l, 26), rt[:, 0:w], ALU.mult, ALU.mult, [bckvf, rtb, bconst], [bckvf])
                ps2, pb2 = G['ring'].next()
                proj(wv, wb, 128, 64, ps2, pb2)
                tt(kpf[0:32, 0:w], ps2[0:32, 0:w], cT[0:32], ALU.mult, [pb2] + tabr, [bkpf])
                tt(kpb_[0:32, 0:w], ps2[32:64, 0:w], sT[32:64], ALU.mult, [pb2] + tabr, [bkpf])
                tt(kpf[0:32, 0:w], kpf[0:32, 0:w], kpb_[0:32, 0:w], ALU.add, [bkpf], [bkpf])
                if not smp:
                    k0 = base + c0
                    ka, kb_ = kct(k0, w)
                    cp(ka, ckvf[:, 0:w], [bckvf], [kb_], eng="act")
                    pa, pb_ = kp4(k0, w)
                    for qd in range(4):
                        cp(pa[qd * 32:(qd + 1) * 32, :], kpf[0:32, 0:w], [bkpf], [pb_], eng="act" if qd % 2 else "dve")
                    split3(ckvf[:, 0:w], [csp[:, i, 0:w] for i in range(3)], [], bckvf, bcsp)
                    split3(kpf[0:32, 0:w], [ksp[0:32, i, 0:w] for i in range(3)], [], bkpf, bksp)
                    KV = int(os.environ.get("KV", "0"))
                    for tb in range(4):
                        ps, pb = G['ring'].next()
                        xT(ps[:, 0:128], [csp[:, i, tb * 128:(tb + 1) * 128] for i in range(3)], 128, [bcsp], [pb])
                        cp(ostg[:, tb, :], ps[:, 0:128], [pb], [bostg])
                        va, vb = vn((k0 // 128) + tb)
                        cp(va, ostg[:, tb, :], [bostg], [vb], eng="act")
                    r0 = (l * NSEQ + seq) * SEQ + k0
                    store(ockv[r0:r0 + 512, :].rearrange("(t p) c -> p t c", p=128), ostg[:, :, :], bostg)
                    import os
                    if os.environ.get("KATT", "1") != "1":
                        pr.op("dve", lambda e: e.memset(merged[:, 4:8, :], 0.0), [], bmg[4:8])
                    if os.environ.get("KATT", "1") == "1":
                      attention_prompt(l, base, c0, w, ti, QL, bQL, (rq, rq3), brq, kct, kp4, vn, Pr, PT, bPT, olT, stat, bstat, Osb, bOsb, On, bOn,
                                       merged, bmg, wuv_v, wkv_b, bgat)
                else:
                    attention_sample(l, cv_s, ckvf, bckvf, kpf, bkpf, QL, bQL, rq, brq, merged, bmg, wuv_v, wkv_b, stat, bstat)
                for g in range(4):
                    wap, wb = wload(wo_d[l * 4 + g])
                    wv = wap.rearrange("p (k m) -> p k m", k=8)
                    for i in range(2):
                        c = g * 2 + i
                        ps, pb = G['ring'].next()
                        for kc in range(8):
                            mm(ps[:, 0:w], wv[:, kc, i * 128:(i + 1) * 128], merged[:, kc, 0:w], kc == 0, kc == 7, [wb, bmg[kc]], [pb])
                        tt(x[:, c, c0:c0 + w], ps[:, 0:w], x[:, c, c0:c0 + w], ALU.add, [pb, bx[ti][c]], [bx[ti][c]])
                if not smp:
                    for tb in range(4):
                        ps2, pb2 = G['ring'].next()
                        xT(ps2[:, 0:32], [ksp[0:32, i, tb * 128:(tb + 1) * 128] for i in range(3)], 32, [bksp], [pb2])
                        cp(kstg[:, tb, :], ps2[:, 0:32], [pb2], [bkstg])
                    r0 = (l * NSEQ + seq) * SEQ + base + c0
                    store(okpe[r0:r0 + 512, :].rearrange("(t p) c -> p t c", p=128), kstg[:, :, :], bkstg)
            if os.environ.get("KDBG"):
                print("mixer carve bytes", smp_phase, cv.off)
            pr.fence()

        def attention_prompt(l, base, c0, w, ti, QL, bQL, rq, brq, kct, kp4, vn, Pr, PT, bPT, olT, stat, bstat, Osb, bOsb, On, bOn,
                             merged, bmg, wuv_v, wkv_b, bgat):
            for qi in range(4):
                q0 = c0 + qi * 128
                aq = (base + q0) // 128
                kext = (aq + 1) * 128
                nkb = (kext + 511) // 512
                ops, opb = psO, bO
                pr.op("dve", lambda e: e.memset(stat[:, 8:32], 0.0), [], [bstat])
                for hd in range(8):
                    qd = (hd % 4) * 32
                    for kb in range(nkb):
                        kw = min(512, kext - kb * 512)
                        ka, kab = kct(kb * 512, kw)
                        pa, pab = kp4(kb * 512, kw)
                        mm(psS[:, kb * 512:kb * 512 + kw], QL[:, hd, q0 - c0:q0 - c0 + 128], ka, True, False, [bQL, kab], [bS[kb]])
                        if qd == 96:
                            mm(psS[:, kb * 512:kb * 512 + kw], rq[1][0:32, hd // 4, q0 - c0:q0 - c0 + 128], pa[0:32, :], False, True,
                               [brq, pab], [bS[kb]])
                        else:
                            mm(psS[:, kb * 512:kb * 512 + kw], rq[0][qd:qd + 32, hd // 4, q0 - c0:q0 - c0 + 128], pa[qd:qd + 32, :], False, True,
                               [brq, pab], [bS[kb]])
                    sall = [bS[kb] for kb in range(nkb)]
                    tt(psS[:, kext - 128:kext], psS[:, kext - 128:kext], maskneg[:], ALU.add, [bS[nkb - 1], bconst], [bS[nkb - 1]])
                    pr.op("dve", lambda e, kext=kext, hd=hd: e.reduce_max(out=stat[:, hd:hd + 1], in_=psS[:, 0:kext], axis=AX.X, negate=True),
                          sall, [bstat])
                    ts(stat[:, hd:hd + 1], stat[:, hd:hd + 1], SCALE, ALU.mult, [bstat], [bstat])
                    P, Pb = Pr.next()
                    act(P[:, 0:kext], psS[:, 0:kext], AF.Exp, sall + [bstat], [Pb, bstat], bias=stat[:, hd:hd + 1], scale=SCALE,
                        accum=stat[:, 8 + hd:9 + hd])
                    nblk = aq + 1
                    for g0 in range(0, nblk, 8):
                        n = min(8, nblk - g0)
                        ps, pb = G['ring'].next()
                        psb_ = ps.bitcast(BF16)
                        for i in range(n):
                            tr(psb_[:, i * 128:(i + 1) * 128], P[:, (g0 + i) * 128:(g0 + i + 1) * 128], ident[:], [Pb, bconst], [pb])
                        cp(PT[:, g0:g0 + n, :], psb_[:, 0:n * 128].rearrange("p (a b) -> p a b", a=n), [pb], [bPT],
                           eng="act" if (g0 // 8) % 2 else "dve")
                    ps, pb = G['ring'].next()
                    for kt in range(nblk):
                        va, vb = vn(kt)
                        mm(ps[:, 0:128], va, PT[:, kt, :], kt == 0, kt == nblk - 1, [vb, bPT], [pb])
                    ol, olb = olT.next()
                    cp(ol[:, :], ps[:, 0:128], [pb], [olb], eng="act")
                    mm(ops[:, hd * 64:(hd + 1) * 64], ol[:, :], wuv_v[:, hd * 64:(hd + 1) * 64], True, True, [olb, wkv_b], [opb])
                recip(stat[:, 16:24], stat[:, 8:16], [bstat], [bstat])
                tt(Osb.rearrange("p (h v) -> p h v", h=8), ops.rearrange("p (h v) -> p h v", h=8),
                   stat[:, 16:24].unsqueeze(2).to_broadcast([128, 8, 64]), ALU.mult, [opb, bstat], [bOsb])
                act(On[:, :], Osb[:, :], AF.Square, [bOsb], [bOn, bstat], accum=stat[:, 24:25])
                act(stat[:, 25:26], stat[:, 24:25], AF.Sqrt, [bstat, bconst], [bstat], bias=cst[:, 0:1], scale=1.0 / 512)
                recip(stat[:, 25:26], stat[:, 25:26], [bstat], [bstat])
                stt(On[:, :], Osb[:, :], stat[:, 25:26], gattn[:, :], ALU.mult, ALU.mult, [bOsb, bstat, bgat], [bOn])
                ps, pb = G['ring'].next()
                psb_ = ps.bitcast(BF16)
                for fc in range(4):
                    tr(psb_[:, fc * 128:(fc + 1) * 128], On[:, fc * 128:(fc + 1) * 128], ident[:], [bOn, bconst], [pb])
                cp(merged[:, 4:8, q0 - c0:q0 - c0 + 128], psb_[:, 0:512].rearrange("p (a b) -> p a b", a=4), [pb], bmg[4:8])

        def attention_sample(l, cv, ckvf, bckvf, kpf, bkpf, QL, bQL, rq, brq, merged, bmg, wuv_v, wkv_b, stat, bstat):
            w = NS
            knat = cv.get([128], BF16); bknat = Buf("knat")
            kcT = cv.get([NS], BF16); kpT = cv.get([NS], BF16); bknew = Buf("knew")
            sstg = cv.get([128], F32); sstg2 = cv.get([32], F32); bsstg = Buf("sstg")
            cp(kcT[:, 0:NS], ckvf[:, 0:NS], [bckvf], [bknew], eng="act")
            cp(kpT[0:32, 0:NS], kpf[0:32, 0:NS], [bkpf], [bknew])
            ssp = cv.get([3, NS], BF16); bssp = Buf("ssp")
            ssp2 = cv.get([3, NS], BF16); bssp2 = Buf("ssp2")
            split3(ckvf[:, 0:NS], [ssp[:, i, :] for i in range(3)], [bknew], bckvf, bssp)
            split3(kpf[0:32, 0:NS], [ssp2[0:32, i, :] for i in range(3)], [bknew], bkpf, bssp2)
            ps, pb = G['ring'].next()
            xT(ps[0:NS, 0:128], [ssp[:, i, :] for i in range(3)], 128, [bssp], [pb])
            cp(sstg[0:NS, :], ps[0:NS, 0:128], [pb], [bsstg])
            cp(knat[0:NS, :], ps[0:NS, 0:128], [pb], [bknat], eng="act")
            store(ockvs[l * NS:(l + 1) * NS, :], sstg[0:NS, :], bsstg)
            ps, pb = G['ring'].next()
            xT(ps[0:NS, 0:32], [ssp2[0:32, i, :] for i in range(3)], 32, [bssp2], [pb])
            cp(sstg2[0:NS, :], ps[0:NS, 0:32], [pb], [bsstg])
            store(okpes[l * NS:(l + 1) * NS, :], sstg2[0:NS, :], bsstg)
            QLz = cv.get([NS, 128], BF16); QPz = cv.get([NS, 128], BF16); bQz = Buf("Qz")
            pr.op("dve", lambda e: e.memset(QLz[:, :, :], 0.0), [], [bQz])
            pr.op("dve", lambda e: e.memset(QPz[:, :, :], 0.0), [], [bQz])
            qlf = QLz.rearrange("p b m -> p (b m)")
            qpf = QPz.rearrange("p b m -> p (b m)")
            pstep = qlf.ap[0][0]
            for hd in range(8):
                dl = bass.AP(qlf.tensor, qlf.offset + hd, [[pstep, 128], [136, NS]])
                cp(dl, QL[:, hd, 0:NS], [bQL], [bQz], eng="act" if hd % 2 else "dve")
                qd = (hd % 4) * 32
                dp = bass.AP(qpf.tensor, qpf.offset + hd, [[pstep, 32], [136, NS]])
                cp(dp, rq[qd:qd + 32, hd // 4, 0:NS], [brq], [bQz], eng="dve" if hd % 2 else "act")
            S = cv.get([NPG * 128 + NS], F32); bSs = Buf("S")
            NK = NPG * 128
            KB = 256
            natr = Ring([(cv.get([NS, 2, 128], BF16), Buf(f"nat{i}")) for i in range(1)])
            kpr = Ring([(cv.get([NS, 2, 32], BF16), Buf(f"kpn{i}")) for i in range(1)])
            mark = cv.off
            cTb = cv.get([NS, KB], BF16); bcTb = Buf("cTb")
            pTb = cv.get([NS, KB], BF16); bpTb = Buf("pTb")
            sck_v = sck.rearrange("(l b j) (t c) -> l t b j c", l=DEPTH, b=NS, c=128)
            skp_v = skp.rearrange("(l b j) (t c) -> l t b j c", l=DEPTH, b=NS, c=32)

            def load_blk(kb):
                nat, nb = natr.next()
                for jj in range(2):
                    load("pool", nat[:, :, jj, :], sck_v[l, :, :, kb * 2 + jj, :], nb, reads=[bgath[l]])
                kpn, kb_ = kpr.next()
                for jj in range(2):
                    load("pool", kpn[:, :, jj, :], skp_v[l, :, :, kb * 2 + jj, :], kb_, reads=[bgath[l]])
                return nat, nb, kpn, kb_

            for kb in range(NK // KB):
                nat, nb, kpn, kpnb = load_blk(kb)
                for g0 in range(0, NS * 2, 8):
                    ps, pb = G['ring'].next()
                    psb_ = ps.bitcast(BF16)
                    for i in range(8):
                        b, jj = (g0 + i) // 2, (g0 + i) % 2
                        tr(psb_[:, i * 128:(i + 1) * 128], nat[:, b, jj, :], ident[:], [nb, bconst], [pb])
                    cp(cTb[:, g0 // 2:g0 // 2 + 4, :], psb_[:, 0:1024].rearrange("p (a b) -> p a b", a=4), [pb], [bcTb],
                       eng="act" if (g0 // 8) % 2 else "dve")
                    ps, pb = G['ring'].next()
                    psb_ = ps.bitcast(BF16)
                    for i in range(8):
                        b, jj = (g0 + i) // 2, (g0 + i) % 2
                        tr(psb_[0:32, i * 128:(i + 1) * 128], kpn[:, b, jj, :], ident[:], [kpnb, bconst], [pb])
                    cp(pTb[0:32, g0 // 2:g0 // 2 + 4, :], psb_[0:32, 0:1024].rearrange("p (a b) -> p a b", a=4), [pb], [bpTb],
                       eng="dve" if (g0 // 8) % 2 else "act")
                ps, pb = G['ring'].next()
                for b in range(NS):
                    mm(ps[:, 0:KB], QLz[:, b, :], cTb[:, b, :], b == 0, False, [bQz, bcTb], [pb])
                    mm(ps[:, 0:KB], QPz[0:32, b, :], pTb[0:32, b, :], False, b == NS - 1, [bQz, bpTb], [pb])
                cp(S[:, kb * KB:(kb + 1) * KB], ps[:, 0:KB], [pb], [bSs], eng="act" if kb % 2 else "dve")
            ps, pb = G['ring'].next()
            for b in range(NS):
                mm(ps[:, 0:NS], QLz[:, b, :], kcT[:, 0:NS], b == 0, False, [bQz, bknew], [pb])
                mm(ps[:, 0:NS], QPz[0:32, b, :], kpT[0:32, 0:NS], False, b == NS - 1, [bQz, bknew], [pb])
            tt(S[:, NK:NK + NS], ps[:, 0:NS], masks[:, :], ALU.add, [pb, bconst], [bSs])
            pr.fence()
            cv.off = mark
            pr.op("dve", lambda e: e.reduce_max(out=stat[:, 0:1], in_=S[:, :], axis=AX.X, negate=True), [bSs], [bstat])
            ts(stat[:, 0:1], stat[:, 0:1], SCALE, ALU.mult, [bstat], [bstat])
            Pm = cv.get([NK + NS], BF16); bPm = Buf("Pm")
            pr.op("dve", lambda e: e.memset(stat[:, 1:2], 0.0), [], [bstat])
            act(Pm[:, :], S[:, :], AF.Exp, [bSs, bstat], [bPm, bstat], bias=stat[:, 0:1], scale=SCALE, accum=stat[:, 1:2])
            recip(stat[:, 2:3], stat[:, 1:2], [bstat], [bstat])
            ts(Pm[:, :], Pm[:, :], stat[:, 2:3], ALU.mult, [bPm, bstat], [bPm])
            PTs = cv.get([2, 128], BF16); bPTs = Buf("PTs")
            PTn = cv.get([128], BF16); bPTn = Buf("PTn")
            ops, opb = psO, bO
            for kb in range(NK // KB):
                nat, nb, kpn, kpnb = load_blk(kb)
                ps, pb = G['ring'].next()
                psb_ = ps.bitcast(BF16)
                for jj in range(2):
                    tr(psb_[:, jj * 128:(jj + 1) * 128], Pm[:, (kb * 2 + jj) * 128:(kb * 2 + jj + 1) * 128], ident[:], [bPm, bconst], [pb])
                cp(PTs[:, :, :], psb_[:, 0:256].rearrange("p (a b) -> p a b", a=2), [pb], [bPTs], eng="act" if kb % 2 else "dve")
                for b in range(NS):
                    for jj in range(2):
                        mm(ops[:, b * 8:(b + 1) * 8], nat[:, b, jj, :], PTs[:, jj, b * 8:(b + 1) * 8], kb == 0 and jj == 0 and b == 0, False,
                           [nb, bPTs], [opb])
            ps, pb = G['ring'].next()
            psb_ = ps.bitcast(BF16)
            tr(psb_[0:NS, 0:128], Pm[:, NK:NK + NS], ident[:], [bPm, bconst], [pb])
            cp(PTn[0:NS, :], psb_[0:NS, 0:128], [pb], [bPTn])
            mm(ops[:, 0:128], knat[0:NS, :], PTn[0:NS, :], False, True, [bknat, bPTn], [opb])
            olz = cv.get([128], BF16); bolz = Buf("olz")
            cp(olz[:, :], ops[:, 0:128], [opb], [bolz])
            aoT = cv.get([4, NS], F32); baoT = Buf("aoT")
            olv = olz.rearrange("p (b h) -> p h b", h=8)
            for i in range(4):
                ps, pb = G['ring'].next()
                for k in range(2):
                    hd = i * 2 + k
                    mm(ps[k * 64:(k + 1) * 64, 0:NS], wuv_v[:, hd * 64:(hd + 1) * 64], olv[:, hd, :], True, True, [wkv_b, bolz], [pb])
                cp(aoT[:, i, :], ps[:, 0:NS], [pb], [baoT], eng="act")
            rt, rtb = rms_stats([aoT[:, i, :] for i in range(4)], NS, 1, [[baoT]] * 4)
            for i in range(4):
                stt(merged[:, 4 + i, 0:NS], aoT[:, i, :], V(l, 43 + i), rt[:, 0:NS], ALU.mult, ALU.mult, [baoT, rtb, bconst], [bmg[4 + i]])

        if do_sample:
            load("sp", pti[:, :], ptd[:, :], bidx)
            cp(idxf[:, 0, :], pti[:, :], [bidx], [bidx])
            for q in range(8):
                ts(idxf[:, 1, :], idxf[:, 0, :], 8.0, ALU.mult, [bidx], [bidx], s2=float(q), op1=ALU.add)
                cp(idx8[:, q, :], idxf[:, 1, :], [bidx], [bidx])
            for q in range(2):
                ts(idxf[:, 1, :], idxf[:, 0, :], 2.0, ALU.mult, [bidx], [bidx], s2=float(q), op1=ALU.add)
                cp(idx8[:, 8 + q, :], idxf[:, 1, :], [bidx], [bidx])
            for l in range(DEPTH):
                for g in range(8):
                    for q in range(8):
                        pending.append((l, 0, g, q))
                    for q in range(2):
                        pending.append((l, 1, g, q))
        make_tables(cst[:, 4:5], cst[:, 5:6], 8192, 1, tmi[:, 0:1], cst[:, 7:8], cst[:, 6:7], btab=bconst)
        for seq in range(NSEQ):
            for half in range(2):
                ws = do_sample and seq == NSEQ - 1 and half == 1
                if ws:
                    while pending:
                        gather_one()
                base = half * 1024
                r0 = seq * SEQ + base
                blocks = [(t // 4, t * 128, xp[r0 + t * 128: r0 + (t + 1) * 128, :], 128) for t in range(8)]
                yblocks = [(t // 4, t * 128, yp[r0 + t * 128: r0 + (t + 1) * 128, :], 128) for t in range(8)]
                if ws:
                    blocks.append((2, 1024, xs[:, :], NS))
                    yblocks.append((2, 1024, ys[:, :], NS))
                load_x(blocks)
                tmpc = Carver()
                make_tables(ctab[:, :], stab[:, :], base, 1024, tmpc.get([1024], I32), tmpc.get([1024], F32), tmpc.get([1024], F32))
                pr.fence()
                tiles = [(0, TILES_P[0]), (1, TILES_P[1])] + ([(2, TS)] if ws else [])
                import os
                KS = int(os.environ.get("KSTAGE", "9"))
                for l in range(DEPTH):
                    if KS >= 1:
                        ffn(tiles, l, 0)
                    if KS >= 2:
                        mixer(l, seq, half, False)
                    if ws and KS >= 3:
                        mixer(l, seq, half, True)
                    if KS >= 1:
                        ffn(tiles, l, 1)
                store_y(yblocks)
        engmap = {"pe": nc.tensor, "act": nc.scalar, "dve": nc.vector, "pool": nc.gpsimd, "sp": nc.sync}
        pr.emit(nc, engmap, st)
    return nc


def _prep_shared(inp, DEPTH):
    f32 = np.float32
    A = lambda k: np.asarray(inp[k], dtype=f32)
    wgu = np.empty((DEPTH * 2 * NFF, 128, 2048), f32)
    wd = np.empty((DEPTH * 2 * 8, 128, NFF * 128), f32)
    for l in range(DEPTH):
        for f, nm in enumerate(("ffn1", "ffn2")):
            for g, part in enumerate(("gate", "up")):
                w = A(f"w_{nm}_{part}")[l].reshape(8, 128, NFF, 128).transpose(2, 1, 0, 3)
                wgu[(l * 2 + f) * NFF:(l * 2 + f + 1) * NFF, :, g * 1024:(g + 1) * 1024] = w.reshape(NFF, 128, 1024)
            w = A(f"w_{nm}_down")[l].reshape(NFF, 128, 8, 128).transpose(2, 1, 0, 3)
            wd[(l * 2 + f) * 8:(l * 2 + f + 1) * 8] = w.reshape(8, 128, NFF * 128)
    sw = (np.arange(32) + 16) % 32
    cols = []
    for c in range(4):
        cols += list(range(512 + c * 128, 512 + (c + 1) * 128)) + list(range(1024 + c * 128, 1024 + (c + 1) * 128))
    cols += list(range(0, 512)) + list(range(1536, 1792)) + list(range(1792, 1920)) + list(range(1920, 1952)) + list(1920 + sw)
    cols = np.array(cols)
    win = np.zeros((DEPTH * 8, 128, 2048), f32)
    wqb = np.empty((DEPTH, 128, 2048), f32)
    wkv = np.empty((DEPTH, 128, 1024), f32)
    wo = np.empty((DEPTH * 4, 128, 2048), f32)
    qcols = np.array([hd * 96 + n for hd in range(8) for n in range(64)] + [hd * 96 + 64 + r for hd in range(8) for r in range(32)]
                     + [hd * 96 + 64 + sw[r] for hd in range(8) for r in range(32)])
    NV = DEPTH * VL + 8
    vecs = np.zeros((128, NV), f32)
    for l in range(DEPTH):
        wp = np.zeros((1024, 2048), f32)
        wp[:, :len(cols)] = A("w_in")[l][:, cols]
        win[l * 8:(l + 1) * 8] = wp.reshape(8, 128, 8, 256).transpose(2, 1, 0, 3).reshape(8, 128, 2048)
        wqb[l] = A("w_q_b")[l][:, qcols].reshape(2, 128, 1024).transpose(1, 0, 2).reshape(128, 2048)
        wk = A("w_kv_b")[l].reshape(128, 8, 128)
        uk = wk[:, :, :64].reshape(128, 4, 2, 64).transpose(2, 3, 1, 0)
        wkv[l, :, :512] = uk.reshape(128, 512)
        wkv[l, :, 512:] = wk[:, :, 64:].reshape(128, 512)
        wo[l * 4:(l + 1) * 4] = A("w_o")[l].reshape(8, 128, 4, 256).transpose(2, 1, 0, 3).reshape(4, 128, 2048)
        b = l * VL
        vecs[:, b:b + 8] = A("norm_ffn1")[l].reshape(8, 128).T
        vecs[:, b + 8:b + 16] = A("norm_mix")[l].reshape(8, 128).T
        vecs[:, b + 16:b + 24] = A("norm_ffn2")[l].reshape(8, 128).T
        vecs[:, b + 24:b + 26] = A("q_a_norm")[l].reshape(2, 128).T
        vecs[:, b + 26] = A("kv_a_norm")[l]
        vecs[:, b + 27:b + 31] = A("gn_conv")[l].reshape(4, 128).T
        vecs[:, b + 31:b + 43] = A("conv_w")[l].reshape(3, 4, 128).transpose(2, 0, 1).reshape(128, 12)
        vecs[:, b + 43:b + 47] = A("gn_attn")[l].reshape(4, 128).T
    vecs[:, DEPTH * VL:] = A("final_norm").reshape(8, 128).T
    npool = inp["cache_ckv"].shape[1]
    return dict(vecs=vecs, gattn=np.ascontiguousarray(A("gn_attn")[:DEPTH]), wgu=wgu, wd=wd, win=win, wqb=wqb, wkv=wkv, wo=wo,
                **{f"cckv{l}": A("cache_ckv")[l].reshape(npool, 128 * 128) for l in range(DEPTH)},
                **{f"ckpe{l}": A("cache_kpe")[l].reshape(npool, 128 * 32) for l in range(DEPTH)})


def run(inp, DEPTH, ncores, do_sample=True):
    npool = inp["cache_ckv"].shape[1]
    nc = build(DEPTH, npool, 2, do_sample)
    shared = _prep_shared(inp, DEPTH)
    in_maps = []
    for c in range(ncores):
        m = dict(shared)
        m["xp"] = np.ascontiguousarray(np.asarray(inp["x_prompt"], np.float32)[2 * c:2 * c + 2]).reshape(2 * SEQ, D)
        m["xs"] = np.ascontiguousarray(np.asarray(inp["x_sample"], np.float32)[NS * c:NS * (c + 1), 0, :])
        m["stc"] = np.ascontiguousarray(np.asarray(inp["state_conv"], np.float32)[:DEPTH, NS * c:NS * (c + 1)]).reshape(DEPTH * NS * 2, 512)
        m["pt"] = np.ascontiguousarray(np.asarray(inp["page_table"], np.int32)[NS * c:NS * (c + 1)]).reshape(8, 128).T.copy()
        in_maps.append(m)
    res = run_bass_kernel_spmd(nc, in_maps, core_ids=list(range(ncores))).results
    cat = lambda k, shp, ax: np.concatenate([r[k].reshape(shp) for r in res], axis=ax)
    return (cat("yp", (2, SEQ, D), 0), cat("ys", (NS, 1, D), 0), cat("ocp", (DEPTH, 2, 2, 512), 1),
            cat("ockv", (DEPTH, 2, SEQ, 128), 1), cat("okpe", (DEPTH, 2, SEQ, 32), 1), cat("ocs", (DEPTH, NS, 2, 512), 1),
            cat("ockvs", (DEPTH, NS, 1, 128), 1), cat("okpes", (DEPTH, NS, 1, 32), 1))


def kernel(**inputs):
    return run(inputs, 4, 8)
```
